# Optimizing a Trainium2 kernel written in Bass

```python
import jax
import jax.numpy as jnp
from jax import lax
import numpy as np

D_MODEL = 2048
BATCH = 1
SEQ = 16384
DEPTH = 2
DEC_BATCH = 16
DEC_SEQ = 64
PAST_LEN = 2048

CHUNK = 64
N_MIXERS = 2
N_CONV_LAYERS = (DEPTH + 1) // 2
N_ATTN_LAYERS = DEPTH // 2
CONV_WIDTH = 3
HEAD_DIM = 64
N_HEADS = D_MODEL // HEAD_DIM
N_KV_HEADS = N_HEADS // 8
GROUP = N_HEADS // N_KV_HEADS
QKV_DIM = (N_HEADS + 2 * N_KV_HEADS) * HEAD_DIM
WINDOW = 128
LOOKBACK_CHUNKS = WINDOW // CHUNK
D_FF = 256 * ((8 * D_MODEL // 3 + 255) // 256)
ROPE_THETA = 10000.0
NORM_EPS = 1e-6
N_NORMS = 6

kernel_name = 'hybrid_stream_conv_swa_macaron_step'


def rms_norm(x, g):
    xf = x.astype(jnp.float32)
    y = xf * lax.rsqrt(jnp.mean(xf * xf, axis=-1, keepdims=True) + NORM_EPS)
    return (y * g.astype(jnp.float32)).astype(x.dtype)


def swiglu(x, w_in, w_out):
    gate, up = jnp.split(x @ w_in, 2, axis=-1)
    return (jax.nn.silu(gate) * up) @ w_out


def rope(x, pos):
    half = HEAD_DIM // 2
    inv_freq = ROPE_THETA ** (-jnp.arange(half, dtype=jnp.float32) / half)
    ang = pos.astype(jnp.float32)[:, None] * inv_freq[None, :]
    cos = jnp.cos(ang)[:, None, :]
    sin = jnp.sin(ang)[:, None, :]
    xf = x.astype(jnp.float32)
    x1, x2 = xf[..., :half], xf[..., half:]
    return jnp.concatenate([x1 * cos - x2 * sin, x2 * cos + x1 * sin], axis=-1).astype(x.dtype)


def sink_softmax(s, sink):
    m = jnp.maximum(jnp.max(s, axis=-1, keepdims=True), sink)
    e = jnp.exp(s - m)
    return e / (jnp.sum(e, axis=-1, keepdims=True) + jnp.exp(sink - m))


def project_qkv(x, w_qkv, pos):
    b, t, _ = x.shape
    qkv = x @ w_qkv
    nq = N_HEADS * HEAD_DIM
    nk = N_KV_HEADS * HEAD_DIM
    q = qkv[..., :nq].reshape(b, t, N_HEADS, HEAD_DIM)
    k = qkv[..., nq:nq + nk].reshape(b, t, N_KV_HEADS, HEAD_DIM)
    v = qkv[..., nq + nk:].reshape(b, t, N_KV_HEADS, HEAD_DIM)
    return rope(q, pos), rope(k, pos), v


def swa_prompt(x, w_qkv, w_o, sinks):
    b, s_len, _ = x.shape
    n_chunks = s_len // CHUNK
    span = (LOOKBACK_CHUNKS + 1) * CHUNK
    q, k, v = project_qkv(x, w_qkv, jnp.arange(s_len))
    pad = ((0, 0), (LOOKBACK_CHUNKS * CHUNK, 0), (0, 0), (0, 0))
    kc = jnp.pad(k, pad).reshape(b, n_chunks + LOOKBACK_CHUNKS, CHUNK, N_KV_HEADS, HEAD_DIM)
    vc = jnp.pad(v, pad).reshape(b, n_chunks + LOOKBACK_CHUNKS, CHUNK, N_KV_HEADS, HEAD_DIM)
    kb = jnp.concatenate([kc[:, j:j + n_chunks] for j in range(LOOKBACK_CHUNKS + 1)], axis=2)
    vb = jnp.concatenate([vc[:, j:j + n_chunks] for j in range(LOOKBACK_CHUNKS + 1)], axis=2)
    qb = q.reshape(b, n_chunks, CHUNK, N_KV_HEADS, GROUP, HEAD_DIM)
    s = jnp.einsum('bcqkgd,bcskd->bckgqs', qb, kb,
                   preferred_element_type=jnp.float32) * (HEAD_DIM ** -0.5)
    key_chunk = (jnp.arange(n_chunks)[:, None] - LOOKBACK_CHUNKS
                 + jnp.arange(span)[None, :] // CHUNK)
    valid = (key_chunk >= 0)[None, :, None, None, None, :]
    s = jnp.where(valid, s, -jnp.inf)
    p = sink_softmax(s, sinks.astype(jnp.float32).reshape(N_KV_HEADS, GROUP, 1, 1))
    o = jnp.einsum('bckgqs,bcskd->bcqkgd', p.astype(vb.dtype), vb)
    o = o.reshape(b, s_len, N_HEADS * HEAD_DIM)
    rows = min(WINDOW, s_len)
    return o @ w_o, k[:, s_len - rows:], v[:, s_len - rows:]


def swa_sample(x, cache_k, cache_v, w_qkv, w_o, sinks):
    b, t, _ = x.shape
    rows = cache_k.shape[1]
    q_pos = PAST_LEN + jnp.arange(t)
    q, k, v = project_qkv(x, w_qkv, q_pos)
    k_all = jnp.concatenate([cache_k.astype(k.dtype), k], axis=1)
    v_all = jnp.concatenate([cache_v.astype(v.dtype), v], axis=1)
    key_pos = PAST_LEN - rows + jnp.arange(rows + t)
    qc = (q_pos // CHUNK)[:, None]
    kc = (key_pos // CHUNK)[None, :]
    valid = (kc >= qc - LOOKBACK_CHUNKS) & (kc <= qc)
    qg = q.reshape(b, t, N_KV_HEADS, GROUP, HEAD_DIM)
    s = jnp.einsum('btkgd,bskd->bkgts', qg, k_all,
                   preferred_element_type=jnp.float32) * (HEAD_DIM ** -0.5)
    s = jnp.where(valid, s, -jnp.inf)
    p = sink_softmax(s, sinks.astype(jnp.float32).reshape(N_KV_HEADS, GROUP, 1, 1))
    o = jnp.einsum('bkgts,bskd->btkgd', p.astype(v_all.dtype), v_all)
    o = o.reshape(b, t, N_HEADS * HEAD_DIM)
    return o @ w_o, k_all[:, -rows:], v_all[:, -rows:]


def short_conv_mixer(x, conv_state, w_in, conv_w, w_out):
    t = x.shape[1]
    b_gate, c_gate, h = jnp.split(x @ w_in, 3, axis=-1)
    u = jnp.concatenate([conv_state.astype(x.dtype), c_gate * h], axis=1)
    conv = sum(u[:, j:j + t] * conv_w[j] for j in range(CONV_WIDTH))
    return (b_gate * conv) @ w_out, u[:, t:]


def trunk(x, conv_state, cache_k, cache_v, norm_g, w_ffn_in, w_ffn_out, w_conv_in,
          w_conv, w_conv_out, w_qkv, w_attn_out, attn_sinks, is_prompt):
    b = x.shape[0]
    conv_new, k_new, v_new = [], [], []
    for i in range(DEPTH):
        g = norm_g[i]
        j = i // N_MIXERS
        x = x + 0.5 * rms_norm(swiglu(rms_norm(x, g[0]), w_ffn_in[i, 0], w_ffn_out[i, 0]), g[1])
        h = rms_norm(x, g[2])
        if i % N_MIXERS == 0:
            if is_prompt:
                st = jnp.zeros((b, CONV_WIDTH - 1, D_MODEL), h.dtype)
            else:
                st = conv_state[j]
            m, st_new = short_conv_mixer(h, st, w_conv_in[j], w_conv[j], w_conv_out[j])
            conv_new.append(st_new)
        else:
            if is_prompt:
                m, kn, vn = swa_prompt(h, w_qkv[j], w_attn_out[j], attn_sinks[j])
            else:
                m, kn, vn = swa_sample(h, cache_k[j], cache_v[j], w_qkv[j], w_attn_out[j], attn_sinks[j])
            k_new.append(kn)
            v_new.append(vn)
        x = x + rms_norm(m, g[3])
        x = x + 0.5 * rms_norm(swiglu(rms_norm(x, g[4]), w_ffn_in[i, 1], w_ffn_out[i, 1]), g[5])
    return x, jnp.stack(conv_new), jnp.stack(k_new), jnp.stack(v_new)


def setup_inputs(seed: int = 0) -> dict:
    key = jax.random.key(seed)
    ks = jax.random.split(key, 14)
    rows = min(WINDOW, PAST_LEN)

    def nrm(k, shape, scale=1.0):
        return jax.random.normal(k, shape, jnp.float32) * scale

    return {
        'x_prompt': nrm(ks[0], (BATCH, SEQ, D_MODEL)),
        'x_sample': nrm(ks[1], (DEC_BATCH, DEC_SEQ, D_MODEL)),
        'state_conv': nrm(ks[2], (N_CONV_LAYERS, DEC_BATCH, CONV_WIDTH - 1, D_MODEL)),
        'cache_k': nrm(ks[3], (N_ATTN_LAYERS, DEC_BATCH, rows, N_KV_HEADS, HEAD_DIM)),
        'cache_v': nrm(ks[4], (N_ATTN_LAYERS, DEC_BATCH, rows, N_KV_HEADS, HEAD_DIM)),
        'norm_g': 1.0 + nrm(ks[5], (DEPTH, N_NORMS, D_MODEL), 0.02),
        'w_ffn_in': nrm(ks[6], (DEPTH, 2, D_MODEL, 2 * D_FF), D_MODEL ** -0.5),
        'w_ffn_out': nrm(ks[7], (DEPTH, 2, D_FF, D_MODEL), D_FF ** -0.5),
        'w_conv_in': nrm(ks[8], (N_CONV_LAYERS, D_MODEL, 3 * D_MODEL), D_MODEL ** -0.5),
        'w_conv': nrm(ks[9], (N_CONV_LAYERS, CONV_WIDTH, D_MODEL), CONV_WIDTH ** -0.5),
        'w_conv_out': nrm(ks[10], (N_CONV_LAYERS, D_MODEL, D_MODEL), D_MODEL ** -0.5),
        'w_qkv': nrm(ks[11], (N_ATTN_LAYERS, D_MODEL, QKV_DIM), D_MODEL ** -0.5),
        'w_attn_out': nrm(ks[12], (N_ATTN_LAYERS, N_HEADS * HEAD_DIM, D_MODEL), (N_HEADS * HEAD_DIM) ** -0.5),
        'attn_sinks': nrm(ks[13], (N_ATTN_LAYERS, N_HEADS)),
    }


def reference(x_prompt, x_sample, state_conv, cache_k, cache_v, norm_g, w_ffn_in, w_ffn_out,
              w_conv_in, w_conv, w_conv_out, w_qkv, w_attn_out, attn_sinks):
    y_prompt, conv_p, k_p, v_p = trunk(x_prompt, None, None, None, norm_g, w_ffn_in, w_ffn_out,
                                       w_conv_in, w_conv, w_conv_out, w_qkv, w_attn_out,
                                       attn_sinks, True)
    y_sample, conv_s, k_s, v_s = trunk(x_sample, state_conv, cache_k, cache_v, norm_g, w_ffn_in,
                                       w_ffn_out, w_conv_in, w_conv, w_conv_out, w_qkv,
                                       w_attn_out, attn_sinks, False)
    return (y_prompt, y_sample, conv_p, conv_s, k_p, v_p, k_s, v_s)
```

```python
import numpy as np
import concourse.bass as bass
import concourse.mybir as mybir
from concourse.bass_utils import run_bass_kernel_spmd

F32 = mybir.dt.float32
BF16 = mybir.dt.bfloat16
AF = mybir.ActivationFunctionType
ALU = mybir.AluOpType
AX = mybir.AxisListType

NCORES = 8
EPS = 1e-6


class Cfg:
    def __init__(self, D=2048, DFF=5632, NH=32, NKV=4, SEQ=16384, DEC_B=16, PAST=2048,
                 theta=10000.0):
        self.D, self.DFF, self.NH, self.NKV = D, DFF, NH, NKV
        self.HD = 64
        self.SEQ, self.DEC_B, self.PAST = SEQ, DEC_B, PAST
        self.theta = theta
        self.KD = D // 128
        self.KF = DFF // 128
        self.GROUP = NH // NKV
        self.QKV = (NH + 2 * NKV) * 64
        self.TPC = SEQ // NCORES
        self.NT = self.TPC // 512
        self.SPC = DEC_B // NCORES
        assert self.SPC == 2 and self.TPC % 512 == 0 and D % 512 == 0
        assert self.KF % 2 == 0 and NH % 8 == 0


class Buf:
    __slots__ = ("w", "r", "name", "kids")

    def __init__(self, name="", kids=()):
        self.w = None
        self.r = []
        self.name = name
        self.kids = tuple(kids)


def _expand(bufs):
    out = []
    for b in bufs:
        out.append(b)
        if b.kids:
            out.extend(b.kids)
    return out


class Op:
    __slots__ = ("eng", "fn", "deps", "signal", "val", "dsem", "dval")

    def __init__(self, eng, fn):
        self.eng, self.fn = eng, fn
        self.deps = ()
        self.signal = False
        self.val = 0
        self.dsem = None
        self.dval = 0


ENGS = ("pe", "act", "dve", "pool", "sp")


class Prog:
    def __init__(self):
        self.ops = {e: [] for e in ENGS}
        self.dma_cnt = {}

    def op(self, eng, fn, reads=(), writes=(), dma_sem=None, after=()):
        o = Op(eng, fn)
        reads = _expand(reads)
        writes = _expand(writes)
        deps = set(after)
        for b in reads:
            if b.w is not None:
                deps.add(b.w)
        for b in writes:
            if b.w is not None:
                deps.add(b.w)
            deps.update(b.r)
        for b in reads:
            b.r.append(o)
        for b in writes:
            b.w = o
            b.r = []
        deps.discard(o)
        if eng == "pe":
            deps = [d for d in deps if not (d.eng == "pe" and d.dsem is None)]
        o.deps = tuple(deps)
        for d in o.deps:
            d.signal = True
        if dma_sem is not None:
            n = self.dma_cnt.get(dma_sem, 0) + 1
            self.dma_cnt[dma_sem] = n
            o.dsem = dma_sem
            o.dval = 16 * n
        self.ops[eng].append(o)
        return o

    def finalize(self):
        for e in ENGS:
            c = 0
            for o in self.ops[e]:
                if o.signal and o.dsem is None:
                    c += 1
                    o.val = c

    def emit(self, eng_name, eng, engsem):
        known = {}
        for o in self.ops[eng_name]:
            need = {}
            for d in o.deps:
                if d.dsem is not None:
                    s, v = d.dsem, d.dval
                else:
                    s, v = engsem[d.eng], d.val
                if need.get(s, 0) < v:
                    need[s] = v
            for s, v in need.items():
                if known.get(s, 0) < v:
                    eng.wait_ge(s, v)
                    known[s] = v
            if o.fn is None:
                continue
            ins = o.fn(eng)
            if o.dsem is not None:
                ins.then_inc(o.dsem, 16)
            elif o.signal:
                ins.then_inc(engsem[eng_name], 1)


class SubTile:
    def __init__(self, idx, col0, rows, kind):
        self.idx, self.col0, self.rows, self.kind = idx, col0, rows, kind


def build_program(cfg, debug_stop=None):
    D, DFF, KD, KF, NH, NKV = cfg.D, cfg.DFF, cfg.KD, cfg.KF, cfg.NH, cfg.NKV
    NPAIR = NH // 2
    NQB = NH // 4
    NXP = cfg.TPC + 130
    TOK = cfg.TPC + 256
    NG = D // 512

    nc = bass.Bass("TRN2", target_bir_lowering=False)

    def din(name, shape):
        return nc.dram_tensor(name, list(shape), F32, kind="ExternalInput").ap()

    def dout(name, shape):
        return nc.dram_tensor(name, list(shape), F32, kind="ExternalOutput").ap()

    xp_d = din("xp", (NXP, D))
    xs_d = din("xs", (128, D))
    sc_d = din("sc", (2, 2, D))
    ck_d = din("ck", (2, 128, NKV * 64))
    cv_d = din("cv", (2, 128, NKV * 64))
    cos_d = din("cos", (TOK, 32))
    sin_d = din("sin", (TOK, 32))
    hmask_d = din("hmask", (128, 1))
    ident_d = din("ident", (128, 128))
    ng_d = din("norm_g", (2, 6, D))
    wfi_d = din("w_ffn_in", (2, 2, D, 2 * DFF))
    wfo_d = din("w_ffn_out", (2, 2, DFF, D))
    wci_d = din("w_conv_in", (1, D, 3 * D))
    wcv_d = din("w_conv", (1, 3, D))
    wco_d = din("w_conv_out", (1, D, D))
    wqkv_d = din("w_qkv", (1, D, cfg.QKV))
    wao_d = din("w_attn_out", (1, D, D))
    snk_d = din("attn_sinks", (1, NH))

    yp_d = dout("y_p", (cfg.TPC, D))
    ys_d = dout("y_s", (128, D))
    cvp_d = dout("conv_p", (2, D))
    cvs_d = dout("conv_s", (2, 2, D))
    kp_d = dout("k_p", (128, NKV * 64))
    vp_d = dout("v_p", (128, NKV * 64))
    ks_d = dout("k_s", (2, 128, NKV * 64))
    vs_d = dout("v_s", (2, 128, NKV * 64))

    P = Prog()
    sb = nc.alloc_sbuf_tensor
    TT = 512

    x_t = sb("x_t", [128, 4, D], F32)
    z_t = sb("z_t", [128, 4, D], F32)
    xnT_t = sb("xnT_t", [128, KD, TT], BF16)
    HCH = max(KF, NH, KD)
    H_t = sb("H_t", [128, HCH, TT], BF16)
    g_t = sb("g_t", [128, 2, D], F32)
    NSLOT = 6
    ring_t = sb("ring_t", [128, NSLOT, 4096], BF16)
    ident_t = sb("ident_t", [128, 128], BF16)
    stat_t = sb("stat_t", [128, 64], F32)
    ssp_t = sb("ssp_t", [128, 4, 4], F32)
    wc_t = sb("wc_t", [128, KD, 3], F32)
    carry_t = sb("carry_t", [128, KD, 2], F32)
    scar_t = sb("scar_t", [128, 2, KD, 2], F32)
    sink_t = sb("sink_t", [128, NPAIR], F32)
    nsink_t = sb("nsink_t", [128, NPAIR], F32)
    hmask_t = sb("hmask_t", [128, 1], F32)
    cs_t = sb("cs_t", [128, 4, 2, 32], F32)
    kT_t = sb("kT_t", [64, NKV, 128 + TT], BF16)
    kTc_t = sb("kTc_t", [64, 2, NKV, 128], BF16)
    VA_t = sb("VA_t", [128, 5, NKV * 64], BF16)
    VO_t = sb("VO_t", [64, 5, NKV * 64], BF16)
    Vc_t = sb("Vc_t", [128, 2, NKV * 64], BF16)
    kvf_t = sb("kvf_t", [128, 1, 2, NKV * 64], F32)
    qr_t = sb("qr_t", [128, 2, 256], BF16)
    PT_t = sb("PT_t", [128, 3, 2, 128], BF16)

    if D >= 2048:
        scr = z_t[:, 2:4, :].rearrange("p a d -> p (a d)")
    else:
        scr_t = sb("scr_t", [128, 4096], F32)
        scr = scr_t[:, :]

    KVW = NKV * 64
    ckf_v = scr[:, 0:2 * KVW].rearrange("p (a c) -> p a c", c=KVW)
    cvf_v = scr[:, 512:512 + 2 * KVW].rearrange("p (a c) -> p a c", c=KVW)
    ckb_v = scr[:, 1024:1024 + KVW].bitcast(BF16).rearrange("p (a c) -> p a c", c=KVW)
    rt_v = scr[:, 0:1024].rearrange("p (b i c) -> p b i c", b=2, i=4)
    NJ = D // 512
    junk_v = H_t[:, 0:NJ, :].rearrange("p a t -> p (a t)")
    xs_v = [H_t[:, NJ * (1 + b):NJ * (2 + b), :].rearrange("p a t -> p (a t)") for b in range(2)]

    psum = [nc.alloc_psum_tensor(f"ps{i}", [128, 512], F32) for i in range(8)]

    xB = [Buf(f"x{i}") for i in range(4)]
    sgB = [Buf("sg0"), Buf("sg1")]
    cvB = [Buf("cvs0"), Buf("cvs1")]
    EB = [Buf(f"E{i}") for i in range(4)]
    PbB = [Buf(f"Pb{i}") for i in range(4)]
    cacheB = Buf("cachestage")
    rtB = [Buf("rt0"), Buf("rt1")]
    zB = [Buf("z0"), Buf("z1"), Buf("z2", kids=cvB + [cacheB] + rtB),
          Buf("z3", kids=sgB + cvB + EB + PbB)]
    xnB = Buf("xnT")
    HB = [Buf(f"H{i}") for i in range(HCH)]
    gB = [Buf("g0"), Buf("g1")]
    ringB = [Buf(f"ring{i}") for i in range(NSLOT)]
    constB = Buf("const")
    statB = [Buf(f"stat{i}") for i in range(64)]
    sspB = [Buf(f"ssp{i}") for i in range(4)]
    carryB = Buf("carry")
    scarB = Buf("scar")
    csB = Buf("cs")
    kTB = Buf("kT")
    kTcB = Buf("kTc")
    VAB = [Buf(f"VA{i}") for i in range(5)]
    VOB = [Buf(f"VO{i}") for i in range(5)]
    VcB = Buf("Vc")
    kvfB = [Buf("kvf0")]
    qrB = [Buf("qr0"), Buf("qr1")]
    PTB = [Buf(f"PT{i}") for i in range(3)]
    psB = [Buf(f"ps{i}") for i in range(8)]

    sems = {}

    def sem(name):
        if name not in sems:
            sems[name] = nc.alloc_semaphore(name)
        return sems[name]

    engsem = {e: sem("e_" + e) for e in ENGS}
    ringsem = [sem(f"ring{i}") for i in range(NSLOT)]
    out_ops = []

    ring_ctr = [0]

    def wload(dst_fn, src_ap, nel):
        s = ring_ctr[0] % NSLOT
        ring_ctr[0] += 1
        dst = dst_fn(ring_t[:, s, :nel])
        P.op("pool", lambda e, dst=dst, src=src_ap: e.dma_start(out=dst, in_=src),
             writes=[ringB[s]], dma_sem=ringsem[s])
        return s

    def act_copy(out, in_, reads, writes):
        P.op("act", lambda e: e.activation(out=out, in_=in_, func=AF.Copy),
             reads=reads, writes=writes)

    def load_g(buf, layer, idx):
        src = ng_d[layer, idx, :].partition_broadcast(128)
        P.op("sp", lambda e: e.dma_start(out=g_t[:, buf, :], in_=src),
             writes=[gB[buf]], dma_sem=sem(f"g{buf}"))

    stat_ctr = [0]

    def stat():
        i = stat_ctr[0] % 64
        stat_ctr[0] += 1
        return stat_t[:, i:i + 1], statB[i]

    junkB = HB[0:NJ]
    xsBufs = [HB[NJ * (1 + b):NJ * (2 + b)] for b in range(2)]

    def rms_sd(src_ap, rows, srcB):
        ss, ssB = stat()
        sd, sdB = stat()
        P.op("act", lambda e: e.activation(out=junk_v[:rows, :], in_=src_ap, func=AF.Square,
                                           accum_out=ss[:rows]),
             reads=[srcB], writes=junkB + [ssB])
        P.op("act", lambda e: e.activation(out=sd[:rows], in_=ss[:rows], func=AF.Sqrt,
                                           scale=1.0 / D, bias=eps_ap[:rows]),
             reads=[ssB, constB], writes=[sdB])
        return sd, sdB

    def rstd_of(sd, sdB, rows):
        rs, rsB = stat()
        P.op("dve", lambda e: e.reciprocal(out=rs[:rows], in_=sd[:rows]),
             reads=[sdB], writes=[rsB])
        return rs, rsB

    xs_ctr = [0]

    def prenorm(sts):
        sds = [rms_sd(x_t[:st.rows, st.idx, :], st.rows, xB[st.idx]) for st in sts]
        for st, (sd, sdB) in zip(sts, sds):
            r = st.rows
            xa = x_t[:r, st.idx, :]
            rs, rsB = rstd_of(sd, sdB, r)
            b = xs_ctr[0] % 2
            xs_ctr[0] += 1
            xsv = xs_v[b]
            P.op("dve", lambda e, xa=xa, rs=rs, r=r, xsv=xsv: e.scalar_tensor_tensor(
                out=xsv[:r, :], in0=xa, scalar=rs[:r], in1=g_t[:r, 0, :],
                op0=ALU.mult, op1=ALU.mult),
                reads=[xB[st.idx], rsB, gB[0]], writes=xsBufs[b])
            for half in range((KD + 7) // 8):
                bank = (6, 7, 4, 5)[(b * 2 + half) % 4]
                pv = psum[bank][:].bitcast(BF16)
                nk = min(8, KD - half * 8)
                for kk in range(nk):
                    k = half * 8 + kk
                    P.op("pe", lambda e, pv=pv, kk=kk, k=k, r=r, xsv=xsv: e.transpose(
                        out=pv[:, kk * 128:kk * 128 + r],
                        in_=xsv[:r, k * 128:(k + 1) * 128], identity=ident_t[:r, :r]),
                        reads=xsBufs[b] + [constB], writes=[psB[bank]])
                src = pv[:, :nk * 128].rearrange("p (k t) -> p k t", t=128)[:, :, :r]
                dst = xnT_t[:, half * 8:half * 8 + nk, st.col0:st.col0 + r]
                act_copy(dst, src, [psB[bank]], [xnB])

    def scale_g(scale):
        if scale != 1.0:
            P.op("dve", lambda e: e.tensor_scalar(out=g_t[:, 1, :], in0=g_t[:, 1, :],
                                                  scalar1=float(scale), scalar2=None,
                                                  op0=ALU.mult),
                 reads=[gB[1]], writes=[gB[1]])

    def postnorm_residual(sts, scale, to_z=False):
        sds = []
        for st in sts:
            r = st.rows
            ss, ssB = stat()
            sd, sdB = stat()
            P.op("dve", lambda e, ss=ss, r=r, st=st: e.tensor_reduce(
                out=ss[:r], in_=ssp_t[:r, st.idx, 0:NG], axis=AX.X, op=ALU.add),
                reads=[sspB[st.idx]], writes=[ssB])
            P.op("act", lambda e, ss=ss, sd=sd, r=r: e.activation(
                out=sd[:r], in_=ss[:r], func=AF.Sqrt, scale=1.0 / D, bias=eps_ap[:r]),
                reads=[ssB, constB], writes=[sdB])
            sds.append((sd, sdB))
        for st, (sd, sdB) in zip(sts, sds):
            r = st.rows
            rs, rsB = rstd_of(sd, sdB, r)
            za = z_t[:r, st.idx, :]
            xa = x_t[:r, st.idx, :]
            if to_z:
                P.op("dve", lambda e, za=za, xa=xa, rs=rs, r=r: e.scalar_tensor_tensor(
                    out=za, in0=za, scalar=rs[:r], in1=xa, op0=ALU.mult, op1=ALU.add),
                    reads=[xB[st.idx], rsB], writes=[zB[st.idx]])
            else:
                P.op("dve", lambda e, za=za, xa=xa, rs=rs, r=r: e.scalar_tensor_tensor(
                    out=xa, in0=za, scalar=rs[:r], in1=xa, op0=ALU.mult, op1=ALU.add),
                    reads=[zB[st.idx], rsB], writes=[xB[st.idx]])

    po_flip = [0]

    def out_proj(w_ap, nk, lhs_fn, lhsB_fn, sts):
        KG = 8
        ngrp = (nk + KG - 1) // KG
        for n in range(NG):
            base = 4 if po_flip[0] % 2 == 0 else 0
            po_flip[0] += 1
            for kg in range(ngrp):
                k0 = kg * KG
                kn = min(KG, nk - k0)
                src = w_ap[k0 * 128:(k0 + kn) * 128, n * 512:(n + 1) * 512].rearrange(
                    "(k p) c -> p k c", p=128)
                s = wload(lambda sl, kn=kn: sl.rearrange("p (k c) -> p k c", c=512), src,
                          kn * 512)
                for st in sts:
                    r = st.rows
                    bank = base + st.idx
                    for i in range(kn):
                        k = k0 + i
                        P.op("pe", lambda e, bank=bank, r=r, k=k, i=i, s=s, st=st:
                             e.matmul(psum[bank][:r, :], lhsT=lhs_fn(k, st),
                                      rhs=ring_t[:, s, i * 512:(i + 1) * 512],
                                      start=(k == 0), stop=(k == nk - 1)),
                             reads=[lhsB_fn(k), ringB[s]], writes=[psB[bank]])
            for st in sts:
                r = st.rows
                bank = base + st.idx
                jk = PT_t[:, :, :, :].rearrange("p a b c -> p (a b c)")[:r, 0:512]
                o1 = P.op("act", lambda e, jk=jk, bank=bank, r=r, st=st, n=n: e.activation(
                    out=jk, in_=psum[bank][:r, :], func=AF.Square,
                    accum_out=ssp_t[:r, st.idx, n:n + 1]),
                    reads=[psB[bank]], writes=[PTB[0], PTB[1], sspB[st.idx]])
                P.op("dve", lambda e, bank=bank, r=r, st=st, n=n: e.tensor_tensor(
                    out=z_t[:r, st.idx, n * 512:(n + 1) * 512], in0=psum[bank][:r, :],
                    in1=g_t[:r, 1, n * 512:(n + 1) * 512], op=ALU.mult),
                    reads=[psB[bank], gB[1]], writes=[zB[st.idx]], after=[o1])

    def ffn(layer, which, sts, ca, cb, final=False):
        N = cb - ca
        load_g(0, layer, 0 if which == 0 else 4)
        load_g(1, layer, 1 if which == 0 else 5)
        scale_g(0.5)
        prenorm(sts)
        w_in = wfi_d[layer, which]
        w_out = wfo_d[layer, which]
        sg_view = scr[:, 2048:2048 + 2 * TT].rearrange("p (b t) -> p b t", t=TT)
        for blk in range(KF // 2):
            slots = []
            for gu in range(2):
                c0 = gu * DFF + blk * 256
                src = w_in[:, c0:c0 + 256].rearrange("(k p) c -> p k c", p=128)
                slots.append(wload(lambda sl: sl.rearrange("p (k c) -> p k c", c=256), src,
                                   KD * 256))
            for a in range(2):
                f = blk * 2 + a
                pb = (f % 2) * 2
                for gu in range(2):
                    s = slots[gu]
                    wv = ring_t[:, s, :KD * 256].rearrange("p (k c) -> p k c", c=256)
                    for k in range(KD):
                        P.op("pe", lambda e, wv=wv, k=k, a=a, pbk=pb + gu: e.matmul(
                            psum[pbk][:, :N], lhsT=wv[:, k, a * 128:(a + 1) * 128],
                            rhs=xnT_t[:, k, ca:cb], start=(k == 0), stop=(k == KD - 1)),
                            reads=[ringB[s], xnB], writes=[psB[pb + gu]])
                sgv = sg_view[:, f % 2, :N]
                P.op("act", lambda e, sgv=sgv, pb=pb: e.activation(
                    out=sgv, in_=psum[pb][:, :N], func=AF.Silu),
                    reads=[psB[pb]], writes=[sgB[f % 2]])
                P.op("dve", lambda e, sgv=sgv, pb=pb, f=f: e.tensor_tensor(
                    out=H_t[:, f, ca:cb], in0=sgv, in1=psum[pb + 1][:, :N], op=ALU.mult),
                    reads=[sgB[f % 2], psB[pb + 1]], writes=[HB[f]])
        out_proj(w_out, KF, lambda k, st: H_t[:, k, st.col0:st.col0 + st.rows],
                 lambda k: HB[k], sts)
        postnorm_residual(sts, 0.5, to_z=final)

    def conv_mixer(sts_in, sts_out, ca, cb, segs, is_first):
        N = cb - ca
        load_g(0, 0, 2)
        load_g(1, 0, 3)
        prenorm(sts_in)
        w_in = wci_d[0]
        zs = scr
        hs_v = zs[:, 0:2 * TT].rearrange("p (b t) -> p b t", t=TT)
        PADW = TT + 8
        up_v = zs[:, 2 * TT:2 * TT + 2 * PADW].rearrange("p (b t) -> p b t", t=PADW)
        ac_v = zs[:, 2 * TT + 2 * PADW:2 * TT + 4 * PADW].rearrange("p (b t) -> p b t", t=PADW)
        offs = []
        o = 0
        for (a, b, kind) in segs:
            offs.append(o)
            o += (b - a) + 2
        Lp = o
        for jp in range(KD // 2):
            slots = []
            for part in range(3):
                c0 = part * D + jp * 256
                src = w_in[:, c0:c0 + 256].rearrange("(k p) c -> p k c", p=128)
                slots.append(wload(lambda sl: sl.rearrange("p (k c) -> p k c", c=256), src,
                                   KD * 256))
            for a2 in range(2):
                j = jp * 2 + a2
                pb = (j % 2) * 4
                for part in range(3):
                    s = slots[part]
                    wv = ring_t[:, s, :KD * 256].rearrange("p (k c) -> p k c", c=256)
                    for k in range(KD):
                        P.op("pe", lambda e, wv=wv, k=k, a2=a2, bk=pb + part: e.matmul(
                            psum[bk][:, :N], lhsT=wv[:, k, a2 * 128:(a2 + 1) * 128],
                            rhs=xnT_t[:, k, ca:cb], start=(k == 0), stop=(k == KD - 1)),
                            reads=[ringB[s], xnB], writes=[psB[pb + part]])
                bi = j % 2
                scrB = [cvB[bi]]
                hs = hs_v[:, bi, :N]
                act_copy(hs, psum[pb + 2][:, :N], [psB[pb + 2]], scrB)
                up = up_v[:, bi, :]
                ac = ac_v[:, bi, :]
                for si, (a, b, kind) in enumerate(segs):
                    o = offs[si]
                    L = b - a
                    if kind == 'p':
                        cin = carry_t[:, j, :]
                        cinB = carryB
                    else:
                        cin = scar_t[:, 0 if kind == 'sA' else 1, j, :]
                        cinB = scarB
                    P.op("act", lambda e, up=up, o=o, cin=cin: e.activation(
                        out=up[:, o:o + 2], in_=cin, func=AF.Copy),
                        reads=[cinB], writes=scrB)
                    P.op("dve", lambda e, up=up, o=o, L=L, hs=hs, a=a, b=b, pb=pb:
                         e.tensor_tensor(out=up[:, o + 2:o + 2 + L], in0=hs[:, a - ca:b - ca],
                                         in1=psum[pb + 1][:, a - ca:b - ca], op=ALU.mult),
                         reads=[psB[pb + 1]] + scrB, writes=scrB)
                n_out = Lp - 2
                P.op("dve", lambda e, ac=ac, up=up, j=j, n_out=n_out: e.tensor_scalar(
                    out=ac[:, :n_out], in0=up[:, 2:2 + n_out], scalar1=wc_t[:, j, 2:3],
                    scalar2=None, op0=ALU.mult),
                    reads=scrB + [constB], writes=scrB)
                for t in (1, 0):
                    P.op("dve", lambda e, ac=ac, up=up, j=j, t=t, n_out=n_out:
                         e.scalar_tensor_tensor(out=ac[:, :n_out], in0=up[:, t:t + n_out],
                                                scalar=wc_t[:, j, t:t + 1], in1=ac[:, :n_out],
                                                op0=ALU.mult, op1=ALU.add),
                         reads=scrB + [constB], writes=scrB)
                for si, (a, b, kind) in enumerate(segs):
                    o = offs[si]
                    L = b - a
                    P.op("dve", lambda e, ac=ac, o=o, L=L, a=a, b=b, pb=pb, j=j:
                         e.tensor_tensor(out=H_t[:, j, a:b], in0=ac[:, o:o + L],
                                         in1=psum[pb][:, a - ca:b - ca], op=ALU.mult),
                         reads=scrB + [psB[pb]], writes=[HB[j]])
                    if kind == 'p':
                        cout, coutB = carry_t[:, j, :], carryB
                    else:
                        cout, coutB = scar_t[:, 0 if kind == 'sA' else 1, j, :], scarB
                    P.op("act", lambda e, up=up, o=o, L=L, cout=cout: e.activation(
                        out=cout, in_=up[:, o + L:o + L + 2], func=AF.Copy),
                        reads=scrB, writes=[coutB])
        if is_first:
            for sq in range(2):
                for r2 in range(2):
                    dst = cvs_d[sq, r2].rearrange("(k p) -> p k", p=128)
                    out_ops.append(P.op("sp", lambda e, dst=dst, sq=sq, r2=r2: e.dma_start(
                        out=dst, in_=scar_t[:, sq, :, r2], allow_slow_non_contiguous=True),
                        reads=[scarB], dma_sem=sem("o_cvs")))
        out_proj(wco_d[0], KD, lambda k, st: H_t[:, k, st.col0:st.col0 + st.rows],
                 lambda k: HB[k], sts_out)
        postnorm_residual(sts_out, 1.0)

    def rope(src_ps, srcB, rows, nh, cs_idx, out_ap, outB, rb):
        sv = src_ps.rearrange("p (h t f) -> p h t f", t=2, f=32)
        ov = out_ap.rearrange("p (h t f) -> p h t f", t=2, f=32)
        x1, x2 = sv[:, :, 0, :], sv[:, :, 1, :]
        cos = cs_t[:rows, cs_idx, 0:1, :].broadcast_to([rows, nh, 32])
        sin = cs_t[:rows, cs_idx, 1:2, :].broadcast_to([rows, nh, 32])
        t = [rt_v[:rows, rb, i, :nh * 32].rearrange("p (h f) -> p h f", f=32) for i in range(4)]
        rd = [srcB, csB]
        P.op("dve", lambda e: e.tensor_tensor(out=t[0], in0=x1, in1=cos, op=ALU.mult),
             reads=rd, writes=[rtB[rb]])
        P.op("dve", lambda e: e.tensor_tensor(out=t[1], in0=x2, in1=sin, op=ALU.mult),
             reads=rd, writes=[rtB[rb]])
        P.op("dve", lambda e: e.tensor_tensor(out=t[2], in0=x2, in1=cos, op=ALU.mult),
             reads=rd, writes=[rtB[rb]])
        P.op("dve", lambda e: e.tensor_tensor(out=t[3], in0=x1, in1=sin, op=ALU.mult),
             reads=rd, writes=[rtB[rb]])
        P.op("dve", lambda e: e.tensor_tensor(out=ov[:, :, 0, :], in0=t[0], in1=t[1],
                                              op=ALU.subtract),
             reads=[rtB[rb]], writes=[outB])
        P.op("dve", lambda e: e.tensor_tensor(out=ov[:, :, 1, :], in0=t[2], in1=t[3],
                                              op=ALU.add),
             reads=[rtB[rb]], writes=[outB])

    kvf_ctr = [0]

    def attention(sts_kv, sts_q, tile_i, tok0, is_first, is_last, last_idx):
        load_g(0, 1, 2)
        load_g(1, 1, 3)
        prenorm(sts_kv)
        wq = wqkv_d[0]
        for st in sts_kv:
            r0 = tok0[st.idx]
            for t2, tab in enumerate((cos_d, sin_d)):
                P.op("sp", lambda e, st=st, r0=r0, t2=t2, tab=tab: e.dma_start(
                    out=cs_t[:st.rows, st.idx, t2, :], in_=tab[r0:r0 + st.rows, :]),
                    writes=[csB], dma_sem=sem("cs"))
        qidx = {st.idx for st in sts_q}
        blocks = ["k", "v"] + list(range(NQB))
        pq_ctr = 0
        deferred = []
        for blk in blocks:
            if blk == "k":
                c0 = NH * 64
            elif blk == "v":
                c0 = NH * 64 + NKV * 64
            else:
                c0 = blk * 256
            ncol = 256 if blk not in ("k", "v") else NKV * 64
            sts_b = sts_kv if blk in ("k", "v") else sts_q
            if not sts_b:
                continue
            src = wq[:, c0:c0 + ncol].rearrange("(k p) c -> p k c", p=128)
            s = wload(lambda sl, ncol=ncol: sl.rearrange("p (k c) -> p k c", c=ncol), src,
                      KD * ncol)
            wv = ring_t[:, s, :KD * ncol].rearrange("p (k c) -> p k c", c=ncol)
            for st in sts_b:
                r = st.rows
                bank = pq_ctr % 4
                pq_ctr += 1
                for k in range(KD):
                    P.op("pe", lambda e, bank=bank, r=r, k=k, wv=wv, st=st, ncol=ncol: e.matmul(
                        psum[bank][:r, :ncol], lhsT=xnT_t[:, k, st.col0:st.col0 + r],
                        rhs=wv[:, k, :], start=(k == 0), stop=(k == KD - 1)),
                        reads=[xnB, ringB[s]], writes=[psB[bank]])
                pa = psum[bank][:r, :ncol]
                while deferred:
                    deferred.pop(0)()
                if blk == "v":
                    act_copy(VA_t[:r, st.idx + 1, :], pa, [psB[bank]], [VAB[st.idx + 1]])
                    need_out = (st.kind == 's') or (is_last and st.idx == last_idx)
                    if need_out:
                        kb = 0
                        act_copy(kvf_t[:r, kb, 1, :], pa, [psB[bank]], [kvfB[kb]])
                        emit_kv_out(st, kb, 1)
                    bank2 = pq_ctr % 4
                    pq_ctr += 1
                    for k in range(KD):
                        P.op("pe", lambda e, bank2=bank2, k=k, wv=wv, st=st, ncol=ncol: e.matmul(
                            psum[bank2][:64, :ncol],
                            lhsT=xnT_t[:, k, st.col0 + 64:st.col0 + 128],
                            rhs=wv[:, k, :], start=(k == 0), stop=(k == KD - 1)),
                            reads=[xnB, ringB[s]], writes=[psB[bank2]])
                    act_copy(VO_t[:64, st.idx + 1, :], psum[bank2][:64, :ncol], [psB[bank2]],
                             [VOB[st.idx + 1]])
                    continue
                rb = pq_ctr % 2
                if blk == "k":
                    need_out = (st.kind == 's') or (is_last and st.idx == last_idx)
                    kb = 0
                    kv_stage[st.idx] = kb
                    rope(pa, psB[bank], r, NKV, st.idx, kvf_t[:r, kb, 0, :], kvfB[kb], rb)
                    P.op("dve", lambda e, r=r, kb=kb, rb=rb: e.tensor_copy(
                        out=qr_t[:r, rb, :NKV * 64], in_=kvf_t[:r, kb, 0, :]),
                        reads=[kvfB[kb]], writes=[qrB[rb]])
                    if need_out:
                        emit_kv_out(st, kb, 0)
                    nh = NKV
                else:
                    rope(pa, psB[bank], r, 4, st.idx, qr_t[:r, rb, :], qrB[rb], rb)
                    nh = 4
                def do_T(blk=blk, st=st, r=r, rb=rb, nh=nh, tbank=4 + (pq_ctr % 2)):
                    pv = psum[tbank][:].bitcast(BF16)
                    for h in range(nh):
                        P.op("pe", lambda e, pv=pv, h=h, r=r, rb=rb: e.transpose(
                            out=pv[:64, h * 128:h * 128 + r], in_=qr_t[:r, rb, h * 64:(h + 1) * 64],
                            identity=ident_t[:r, :r]),
                            reads=[qrB[rb], constB], writes=[psB[tbank]])
                    srcv = pv[:64, :nh * 128].rearrange("p (h t) -> p h t", t=128)[:, :, :r]
                    if blk == "k":
                        dst = kT_t[:, :, 128 + st.col0:128 + st.col0 + r]
                        act_copy(dst, srcv, [psB[tbank]], [kTB])
                    else:
                        m = st.col0 // 128
                        for pp in range(2):
                            jj = blk * 2 + pp
                            pf = H_t[:64, 2 * jj:2 * jj + 2, :].rearrange("p a t -> p (a t)")
                            dst = pf[:, m * 256:(m + 1) * 256].rearrange(
                                "p (e h q) -> p h e q", e=2, h=2)
                            srcp = pv[:64, pp * 256:(pp + 1) * 256].rearrange(
                                "p (h e q) -> p h e q", h=2, e=2)
                            act_copy(dst, srcp, [psB[tbank]], [HB[2 * jj], HB[2 * jj + 1]])
                deferred.append(do_T)
        while deferred:
            deferred.pop(0)()

        GP = min(8, NPAIR)

        def unit_A(u):
            st, e_idx, parts, qcol, mask_cols, j, ui = u
            nkeys = sum(p[2] for p in parts)
            g = (2 * j) // cfg.GROUP
            slot = ui % 4
            bank = (4, 5, 0)[ui % 3]
            Sap = psum[bank][:, :nkeys]
            ci = (st.col0 // 128) * 2 + e_idx
            lhs = H_t[:64, 2 * j:2 * j + 2, :].rearrange("p a t -> p (a t)")[
                :, ci * 128:(ci + 1) * 128]
            off = 0
            if st.kind != 's':
                k_lo = qcol
                P.op("pe", lambda e, Sap=Sap, lhs=lhs, g=g, k_lo=k_lo, nkeys=nkeys:
                     e.matmul(Sap[:, :nkeys], lhsT=lhs, rhs=kT_t[:, g, k_lo:k_lo + nkeys],
                              start=True, stop=True),
                     reads=[HB[2 * j], HB[2 * j + 1], kTB], writes=[psB[bank]])
            else:
                for (kfn, vfn, Kp, bufs) in parts:
                    P.op("pe", lambda e, Sap=Sap, lhs=lhs, kfn=kfn, g=g, off=off, Kp=Kp:
                         e.matmul(Sap[:, off:off + Kp], lhsT=lhs, rhs=kfn(g),
                                  start=True, stop=True),
                         reads=[HB[2 * j], HB[2 * j + 1]] + bufs, writes=[psB[bank]])
                    off += Kp
            u_state[ui] = dict(Sap=Sap, g=g, slot=slot, nkeys=nkeys, SBk=psB[bank],
                               tbank=6 + ui % 2)

        def unit_B(u):
            st, e_idx, parts, qcol, mask_cols, j, ui = u
            d = u_state[ui]
            Sap, slot, SBk = d["Sap"], d["slot"], d["SBk"]
            rmax, rmaxB = stat()
            negm, negmB = stat()
            if mask_cols:
                P.op("dve", lambda e, Sap=Sap: e.tensor_scalar(
                    out=Sap[:, :mask_cols], in0=Sap[:, :mask_cols], scalar1=hmask_t[:, 0:1],
                    scalar2=None, op0=ALU.add),
                    reads=[constB], writes=[SBk])
            P.op("dve", lambda e, Sap=Sap, rmax=rmax: e.tensor_reduce(
                out=rmax, in_=Sap, axis=AX.X, op=ALU.max),
                reads=[SBk], writes=[rmaxB])
            P.op("dve", lambda e, rmax=rmax, negm=negm, j=j: e.tensor_scalar(
                out=negm, in0=rmax, scalar1=-0.125, scalar2=nsink_t[:, j:j + 1],
                op0=ALU.mult, op1=ALU.min),
                reads=[rmaxB, constB], writes=[negmB])
            d.update(negm=negm, negmB=negmB)

        def unit_C(u):
            st, e_idx, parts, qcol, mask_cols, j, ui = u
            d = u_state[ui]
            Sap, slot, nkeys, negm, negmB = d["Sap"], d["slot"], d["nkeys"], d["negm"], d["negmB"]
            E = scr[:, 2048 + slot * 192:2048 + slot * 192 + nkeys]
            es, esB = stat()
            rsum, rsumB = stat()
            P.op("act", lambda e, E=E, Sap=Sap, negm=negm, rsum=rsum: e.activation(
                out=E, in_=Sap, func=AF.Exp, bias=negm, scale=0.125, accum_out=rsum),
                reads=[d["SBk"], negmB], writes=[EB[slot], rsumB])
            P.op("act", lambda e, es=es, negm=negm, j=j: e.activation(
                out=es, in_=negm, func=AF.Exp, bias=sink_t[:, j:j + 1], scale=1.0),
                reads=[negmB, constB], writes=[esB])
            d.update(E=E, es=es, esB=esB, rsum=rsum, rsumB=rsumB)

        def unit_D(u):
            st, e_idx, parts, qcol, mask_cols, j, ui = u
            d = u_state[ui]
            slot, nkeys, E, es, esB = d["slot"], d["nkeys"], d["E"], d["es"], d["esB"]
            rsum, rsumB = d["rsum"], d["rsumB"]
            den, denB = stat()
            rden, rdenB = stat()
            Pb = scr[:, 3072:3584].bitcast(BF16)[:, slot * 192:slot * 192 + nkeys]
            P.op("dve", lambda e, den=den, rsum=rsum, es=es: e.tensor_tensor(
                out=den, in0=rsum, in1=es, op=ALU.add),
                reads=[rsumB, esB], writes=[denB])
            P.op("dve", lambda e, den=den, rden=rden: e.reciprocal(out=rden, in_=den),
                 reads=[denB], writes=[rdenB])
            P.op("dve", lambda e, Pb=Pb, E=E, rden=rden: e.tensor_scalar(
                out=Pb, in0=E, scalar1=rden, scalar2=None, op0=ALU.mult),
                reads=[EB[slot], rdenB], writes=[PbB[slot]])
            d.update(Pb=Pb)

        def unit_E(u):
            st, e_idx, parts, qcol, mask_cols, j, ui = u
            d = u_state[ui]
            slot, Pb = d["slot"], d["Pb"]
            tb = d["tbank"]
            tpv = psum[tb][:].bitcast(BF16)[:, 0:256]
            off = 0
            for pi, (kfn, vfn, Kp, bufs) in enumerate(parts):
                P.op("pe", lambda e, tpv=tpv, pi=pi, Kp=Kp, Pb=Pb, off=off: e.transpose(
                    out=tpv[:Kp, pi * 128:(pi + 1) * 128], in_=Pb[:, off:off + Kp],
                    identity=ident_t[:, :]),
                    reads=[PbB[slot], constB], writes=[psB[tb]])
                off += Kp
            d.update(tpv=tpv)

        def unit_F(u):
            st, e_idx, parts, qcol, mask_cols, j, ui = u
            d = u_state[ui]
            slot, tpv = d["slot"], d["tpv"]
            ps3 = ui % 3
            for pi, (kfn, vfn, Kp, bufs) in enumerate(parts):
                act_copy(PT_t[:Kp, ps3, pi, :], tpv[:Kp, pi * 128:(pi + 1) * 128],
                         [psB[d["tbank"]]], [PTB[ps3]])

        def unit_G(u):
            st, e_idx, parts, qcol, mask_cols, j, ui = u
            d = u_state[ui]
            slot, g = d["slot"], d["g"]
            pobank = 2 + (ui // GP) % 2
            for hh in range(2):
                for pi, (kfn, vfn, Kp, bufs) in enumerate(parts):
                    P.op("pe", lambda e, hh=hh, pi=pi, Kp=Kp, vfn=vfn, g=g, ps3=ui % 3, j=j,
                         pobank=pobank:
                         e.matmul(psum[pobank][hh * 64:(hh + 1) * 64,
                                               (j % GP) * 64:(j % GP) * 64 + 64],
                                  lhsT=vfn(g), rhs=PT_t[:Kp, ps3, pi, hh * 64:(hh + 1) * 64],
                                  start=(pi == 0), stop=(pi == len(parts) - 1)),
                         reads=[PTB[ui % 3]] + bufs, writes=[psB[pobank]])
            if j % GP == GP - 1:
                j0 = j - (GP - 1)
                srcv = psum[pobank][:, :GP * 64].rearrange("p (j q) -> p j q", q=64)
                dst = xnT_t[:, j0:j0 + GP, qcol:qcol + 64]
                act_copy(dst, srcv, [psB[pobank]], [xnB])
            del u_state[ui]

        u_state = {}
        units = []

        def chunk_attention(st, e_idx, parts, qcol, mask_cols):
            for j in range(NPAIR):
                units.append((st, e_idx, parts, qcol, mask_cols, j, len(units)))

        for st in sts_q:
            for e_idx in range(2):
                qcol = st.col0 + e_idx * 64
                if st.kind == 's':
                    sq = e_idx
                    parts = [
                        (lambda g, sq=sq: kTc_t[:, sq, g, :],
                         lambda g, sq=sq: Vc_t[:, sq, g * 64:(g + 1) * 64], 128, [kTcB, VcB]),
                        (lambda g, qcol=qcol: kT_t[:, g, 128 + qcol:128 + qcol + 64],
                         lambda g, st=st, e_idx=e_idx: (VA_t if e_idx == 0 else VO_t)[
                             :64, st.idx + 1, g * 64:(g + 1) * 64],
                         64, [kTB, VAB[st.idx + 1], VOB[st.idx + 1]]),
                    ]
                    mask_cols = 0
                else:
                    k_lo = 128 + qcol - 128
                    if e_idx == 0:
                        parts = [
                            (lambda g, k_lo=k_lo: kT_t[:, g, k_lo:k_lo + 128],
                             lambda g, st=st: VA_t[:, st.idx, g * 64:(g + 1) * 64], 128,
                             [kTB, VAB[st.idx]]),
                            (lambda g, k_lo=k_lo: kT_t[:, g, k_lo + 128:k_lo + 192],
                             lambda g, st=st: VA_t[:64, st.idx + 1, g * 64:(g + 1) * 64], 64,
                             [kTB, VAB[st.idx + 1]]),
                        ]
                    else:
                        parts = [
                            (lambda g, k_lo=k_lo: kT_t[:, g, k_lo:k_lo + 64],
                             lambda g, st=st: VO_t[:64, st.idx, g * 64:(g + 1) * 64], 64,
                             [kTB, VOB[st.idx]]),
                            (lambda g, k_lo=k_lo: kT_t[:, g, k_lo + 64:k_lo + 192],
                             lambda g, st=st: VA_t[:, st.idx + 1, g * 64:(g + 1) * 64], 128,
                             [kTB, VAB[st.idx + 1]]),
                        ]
                    mask_cols = 0
                    if getattr(st, 'first_real', False):
                        mask_cols = 128 if e_idx == 0 else 64
                chunk_attention(st, e_idx, parts, qcol, mask_cols)

        nu = len(units)
        stages = [(6, unit_A), (5, unit_B), (4, unit_C), (3, unit_D), (2, unit_E),
                  (1, unit_F), (0, unit_G)]
        for i in range(-6, nu):
            for sk, fn in stages:
                if 0 <= i + sk < nu:
                    fn(units[i + sk])
        if is_first:
            dump("oT_s", xnT_t[:, :, 258:386], [xnB])
            dump("qT_s", H_t[:64, 0:NH, :], [HB[i] for i in range(NH)])
            dump("kT_s", kT_t[:, :, 386:514], [kTB])
            dump("kTc", kTc_t[:, :, :, :], [kTcB])
            dump("VA", VA_t[:, 2, :], [VAB[2]])
            dump("VO", VO_t[:, 2, :], [VOB[2]])
            dump("Vc", Vc_t[:, :, :], [VcB])
        pst = [st for st in sts_kv if st.kind != 's']
        if pst and not is_last:
            last = pst[-1]
            lc = 128 + last.col0
            P.op("act", lambda e, lc=lc: e.activation(
                out=kT_t[:, :, 0:128], in_=kT_t[:, :, lc:lc + 128], func=AF.Copy),
                reads=[kTB], writes=[kTB])
            P.op("act", lambda e, last=last: e.activation(
                out=VA_t[:, 0, :], in_=VA_t[:, last.idx + 1, :], func=AF.Copy),
                reads=[VAB[last.idx + 1]], writes=[VAB[0]])
            P.op("act", lambda e, last=last: e.activation(
                out=VO_t[:, 0, :], in_=VO_t[:, last.idx + 1, :], func=AF.Copy),
                reads=[VOB[last.idx + 1]], writes=[VOB[0]])
        if sts_q:
            out_proj(wao_d[0], KD, lambda k, st: xnT_t[:, k, st.col0:st.col0 + st.rows],
                     lambda k: xnB, sts_q)
            postnorm_residual(sts_q, 1.0)

    kv_stage = {}
    dbg_outs = {}

    def dump(name, ap, bufs):
        if not debug_stop:
            return
        shp = list(ap.shape)
        d = nc.dram_tensor("dbg_" + name, shp, F32, kind="ExternalOutput").ap()
        dbg_outs[name] = d
        out_ops.append(P.op("pool", lambda e: e.dma_start(out=d, in_=ap), reads=bufs,
                            dma_sem=sem("dbg_" + name)))

    def emit_kv_out(st, kb, which):
        outs_s, outs_p = (ks_d, kp_d) if which == 0 else (vs_d, vp_d)
        if st.kind == 's':
            for sq in range(2):
                out_ops.append(P.op("sp", lambda e, sq=sq, kb=kb: e.dma_start(
                    out=outs_s[sq, 64:128, :], in_=kvf_t[sq * 64:(sq + 1) * 64, kb, which, :]),
                    reads=[kvfB[kb]], dma_sem=sem("o_kv")))
        else:
            out_ops.append(P.op("sp", lambda e, kb=kb: e.dma_start(
                out=outs_p[:, :], in_=kvf_t[:, kb, which, :]), reads=[kvfB[kb]],
                dma_sem=sem("o_kv")))

    with nc.allow_low_precision("bf16 matmul operands, fp32 accumulation"):
        pass
    eps_t = sb("eps_t", [128, 1], F32)
    eps_ap = eps_t[:, 0:1]
    P.op("dve", lambda e: e.memset(eps_t[:, :], EPS), writes=[constB])
    P.op("dve", lambda e: e.memset(carry_t[:, :, :], 0.0), writes=[carryB])
    P.op("pool", lambda e: e.dma_start(out=ident_t[:, :], in_=ident_d[:, :]),
         writes=[constB], dma_sem=sem("c0"))
    P.op("sp", lambda e: e.dma_start(out=hmask_t[:, :], in_=hmask_d[:, :]),
         writes=[constB], dma_sem=sem("c1"))
    for j3 in range(3):
        P.op("sp", lambda e, j3=j3: e.dma_start(
            out=wc_t[:, :, j3], in_=wcv_d[0, j3].rearrange("(k p) -> p k", p=128),
            allow_slow_non_contiguous=True), writes=[constB], dma_sem=sem("c2"))
    for half in range(2):
        srcs = snk_d[0, :].rearrange("(j t) -> t j", t=2)[half:half + 1, :]
        P.op("sp", lambda e, half=half, srcs=srcs: e.dma_start(
            out=sink_t[half * 64:(half + 1) * 64, :], in_=srcs.partition_broadcast(64)
            if False else srcs.broadcast_to([64, NPAIR]),
            allow_slow_non_contiguous=True), writes=[constB], dma_sem=sem("c3"))
    P.op("dve", lambda e: e.tensor_scalar(out=nsink_t[:, :], in0=sink_t[:, :], scalar1=-1.0,
                                          scalar2=None, op0=ALU.mult),
         reads=[constB], writes=[constB])
    for sq in range(2):
        for r2 in range(2):
            P.op("sp", lambda e, sq=sq, r2=r2: e.dma_start(
                out=scar_t[:, sq, :, r2], in_=sc_d[sq, r2].rearrange("(k p) -> p k", p=128),
                allow_slow_non_contiguous=True), writes=[scarB], dma_sem=sem("c4"))
    for sq in range(2):
        out_ops.append(P.op("sp", lambda e, sq=sq: e.dma_start(
            out=ks_d[sq, 0:64, :], in_=ck_d[sq, 64:128, :]), dma_sem=sem("o_c")))
        out_ops.append(P.op("sp", lambda e, sq=sq: e.dma_start(
            out=vs_d[sq, 0:64, :], in_=cv_d[sq, 64:128, :]), dma_sem=sem("o_c")))
        P.op("sp", lambda e, sq=sq: e.dma_start(out=ckf_v[:, sq, :], in_=ck_d[sq, :, :]),
             writes=[cacheB], dma_sem=sem("c5"))
        P.op("sp", lambda e, sq=sq: e.dma_start(out=cvf_v[:, sq, :], in_=cv_d[sq, :, :]),
             writes=[cacheB], dma_sem=sem("c5"))
    P.op("dve", lambda e: e.tensor_copy(out=ckb_v[:, :, :], in_=ckf_v[:, :, :]),
         reads=[cacheB], writes=[cacheB])
    P.op("dve", lambda e: e.tensor_copy(out=Vc_t[:, :, :], in_=cvf_v[:, :, :]),
         reads=[cacheB], writes=[VcB])
    for sq in range(2):
        pv = psum[7][:].bitcast(BF16)
        for g in range(NKV):
            P.op("pe", lambda e, pv=pv, sq=sq, g=g: e.transpose(
                out=pv[:64, g * 128:(g + 1) * 128], in_=ckb_v[:, sq, g * 64:(g + 1) * 64],
                identity=ident_t[:, :]), reads=[cacheB, constB], writes=[psB[7]])
        act_copy(kTc_t[:, sq, :, :], pv[:64, :NKV * 128].rearrange("p (g t) -> p g t", t=128),
                 [psB[7]], [kTcB])

    e_st = SubTile(0, 0, 2, 'e')
    h_st = SubTile(2, 2, 128, 'h')
    p_st = SubTile(3, 130, 128, 'p')
    s_st = SubTile(1, 258, 128, 's')
    p_st.first_real = True
    NF = 386
    NFULL = (cfg.TPC - 128 - 384) // 512
    assert 128 + 512 * NFULL + 384 == cfg.TPC

    def load_x(st, src):
        P.op("sp", lambda e: e.dma_start(out=x_t[:st.rows, st.idx, :], in_=src),
             writes=[xB[st.idx]], dma_sem=sem(f"x{st.idx}"))

    def store_y(st, dst):
        out_ops.append(P.op("sp", lambda e: e.dma_start(out=dst, in_=z_t[:st.rows, st.idx, :]),
                            reads=[zB[st.idx]], dma_sem=sem(f"y{st.idx}")))

    def conv_p_out():
        for r2 in range(2):
            dst = cvp_d[r2].rearrange("(k p) -> p k", p=128)
            out_ops.append(P.op("sp", lambda e, dst=dst, r2=r2: e.dma_start(
                out=dst, in_=carry_t[:, :, r2], allow_slow_non_contiguous=True),
                reads=[carryB], dma_sem=sem("o_cvp")))

    load_x(e_st, xp_d[0:2, :])
    load_x(h_st, xp_d[2:130, :])
    load_x(p_st, xp_d[130:258, :])
    load_x(s_st, xs_d[:, :])
    ffn(0, 0, [e_st, h_st, p_st, s_st], 0, NF)
    conv_mixer([e_st, h_st, p_st, s_st], [h_st, p_st, s_st], 0, NF,
               [(0, 258, 'p'), (258, 322, 'sA'), (322, NF, 'sB')], True)
    ffn(0, 1, [h_st, p_st, s_st], 2, NF)
    ffn(1, 0, [h_st, p_st, s_st], 2, NF)
    tok0 = {1: cfg.TPC + 128, 2: 0, 3: 128}
    attention([h_st, p_st, s_st], [p_st, s_st], 0, tok0, True, False, 3)
    ffn(1, 1, [p_st, s_st], 130, NF, final=True)
    store_y(s_st, ys_d[:, :])
    store_y(p_st, yp_d[0:128, :])

    ntiles = NFULL + 1
    for ti in range(1, ntiles + 1):
        is_last = (ti == ntiles)
        nsub = 3 if is_last else 4
        W = nsub * 128
        sts = [SubTile(m, m * 128, 128, 'p') for m in range(nsub)]
        t_real = 128 + (ti - 1) * 512
        for st in sts:
            r0 = 130 + t_real + st.idx * 128
            load_x(st, xp_d[r0:r0 + 128, :])
        ffn(0, 0, sts, 0, W)
        conv_mixer(sts, sts, 0, W, [(0, W, 'p')], False)
        if is_last:
            conv_p_out()
        ffn(0, 1, sts, 0, W)
        ffn(1, 0, sts, 0, W)
        tok0 = {m: 128 + t_real + m * 128 for m in range(nsub)}
        attention(sts, sts, ti, tok0, False, is_last, nsub - 1)
        ffn(1, 1, sts, 0, W, final=True)
        for st in sts:
            y0 = t_real + st.idx * 128
            store_y(st, yp_d[y0:y0 + 128, :])

    P.op("sp", None, after=out_ops)
    P.finalize()

    with nc.allow_low_precision("bf16 matmul operands, fp32 accumulation"):
        with nc.Block() as block:
            @block.tensor
            def _(e):
                P.emit("pe", e, engsem)

            @block.scalar
            def _(e):
                P.emit("act", e, engsem)

            @block.vector
            def _(e):
                P.emit("dve", e, engsem)

            @block.gpsimd
            def _(e):
                P.emit("pool", e, engsem)

            @block.sync
            def _(e):
                P.emit("sp", e, engsem)
    return nc, P


def make_in_maps(cfg, x_prompt, x_sample, state_conv, cache_k, cache_v, norm_g, w_ffn_in,
                 w_ffn_out, w_conv_in, w_conv, w_conv_out, w_qkv, w_attn_out, attn_sinks):
    D = cfg.D
    f32 = np.float32
    xp_full = np.ascontiguousarray(x_prompt, dtype=f32).reshape(cfg.SEQ, D)
    xpad = np.concatenate([np.zeros((130, D), f32), xp_full], axis=0)
    half = 32
    inv_freq = (np.float32(cfg.theta) ** (-np.arange(half, dtype=f32) / np.float32(half))).astype(f32)
    ident = np.eye(128, dtype=f32)
    shared = {
        "ident": ident,
        "norm_g": np.ascontiguousarray(norm_g, dtype=f32),
        "w_ffn_in": np.ascontiguousarray(w_ffn_in, dtype=f32),
        "w_ffn_out": np.ascontiguousarray(w_ffn_out, dtype=f32),
        "w_conv_in": np.ascontiguousarray(w_conv_in, dtype=f32),
        "w_conv": np.ascontiguousarray(w_conv, dtype=f32),
        "w_conv_out": np.ascontiguousarray(w_conv_out, dtype=f32),
        "w_qkv": np.ascontiguousarray(w_qkv, dtype=f32),
        "w_attn_out": np.ascontiguousarray(w_attn_out, dtype=f32),
        "attn_sinks": np.ascontiguousarray(attn_sinks, dtype=f32),
    }
    xs_all = np.ascontiguousarray(x_sample, dtype=f32)
    sc_all = np.ascontiguousarray(state_conv, dtype=f32)
    ck_all = np.ascontiguousarray(cache_k, dtype=f32).reshape(cfg.DEC_B, 128, cfg.NKV * 64)
    cv_all = np.ascontiguousarray(cache_v, dtype=f32).reshape(cfg.DEC_B, 128, cfg.NKV * 64)
    in_maps = []
    for c in range(NCORES):
        t0 = c * cfg.TPC
        pos = np.concatenate([
            np.arange(t0 - 128, t0 + cfg.TPC),
            cfg.PAST + np.arange(64), cfg.PAST + np.arange(64)]).astype(f32)
        ang = pos[:, None] * inv_freq[None, :]
        m = dict(shared)
        m["xp"] = np.ascontiguousarray(xpad[t0:t0 + cfg.TPC + 130])
        m["xs"] = np.ascontiguousarray(xs_all[2 * c:2 * c + 2].reshape(128, D))
        m["sc"] = np.ascontiguousarray(sc_all[0, 2 * c:2 * c + 2])
        m["ck"] = np.ascontiguousarray(ck_all[2 * c:2 * c + 2])
        m["cv"] = np.ascontiguousarray(cv_all[2 * c:2 * c + 2])
        m["cos"] = np.cos(ang).astype(f32)
        m["sin"] = np.sin(ang).astype(f32)
        m["hmask"] = np.full((128, 1), -1.0e6 if c == 0 else 0.0, f32)
        in_maps.append(m)
    return in_maps


def assemble(cfg, results):
    D = cfg.D
    y_p = np.concatenate([r["y_p"] for r in results], axis=0).reshape(1, cfg.SEQ, D)
    y_s = np.concatenate([r["y_s"].reshape(2, 64, D) for r in results], axis=0)
    conv_p = results[-1]["conv_p"].reshape(1, 1, 2, D)
    conv_s = np.concatenate([r["conv_s"] for r in results], axis=0).reshape(1, cfg.DEC_B, 2, D)
    k_p = results[-1]["k_p"].reshape(1, 1, 128, cfg.NKV, 64)
    v_p = results[-1]["v_p"].reshape(1, 1, 128, cfg.NKV, 64)
    k_s = np.concatenate([r["k_s"] for r in results], axis=0).reshape(1, cfg.DEC_B, 128, cfg.NKV, 64)
    v_s = np.concatenate([r["v_s"] for r in results], axis=0).reshape(1, cfg.DEC_B, 128, cfg.NKV, 64)
    return tuple(np.ascontiguousarray(a, dtype=np.float32)
                 for a in (y_p, y_s, conv_p, conv_s, k_p, v_p, k_s, v_s))


def run(cfg, inputs, trace=False):
    nc, _ = build_program(cfg)
    in_maps = make_in_maps(cfg, **inputs)
    res = run_bass_kernel_spmd(nc, in_maps, core_ids=list(range(NCORES)), trace=trace)
    return assemble(cfg, res.results), res


def kernel(**inputs):
    cfg = Cfg()
    outs, _ = run(cfg, inputs)
    return outs
```
